# Optimizing a Trainium2 kernel written in Bass

```python
import jax, jax.numpy as jnp
from jax import lax
import numpy as np

D_MODEL = 4096
BATCH = 1
SEQ = 8192
DEPTH = 4

GRID_W = 64
CTX_LEN = 256
N_MIXERS = 3
NA_HEADS = 32
NA_HEAD_DIM = D_MODEL // NA_HEADS
NA_KH = 8
NA_KW = 16
RET_HEADS = 16
RET_DK = D_MODEL // RET_HEADS
RET_DV = D_MODEL // RET_HEADS
RET_CHUNK = 128
RET_EXP_MIN = 5.0
RET_EXP_MAX = 12.0
ML_HEADS = 8
ML_QK_DIM = D_MODEL // 2
ML_DK = ML_QK_DIM // ML_HEADS
ML_DV = D_MODEL // ML_HEADS
ML_CHUNK = 64
ML_FGATE_BIAS_MIN = 3.0
ML_FGATE_BIAS_MAX = 6.0
FFN_HIDDEN = D_MODEL
CONV_WIDTH = 3
ROPE_BASE = 10000.0
NORM_EPS = 1e-6
NEG_INF = -1e30

kernel_name = "hybrid_natten_retnet_mlstm_dit_block"


def rmsnorm(x, g):
    xf = x.astype(jnp.float32)
    y = xf * lax.rsqrt(jnp.mean(xf * xf, axis=-1, keepdims=True) + NORM_EPS)
    return (y * g.astype(jnp.float32)).astype(x.dtype)


def head_rms(o):
    of = o.astype(jnp.float32)
    return of * lax.rsqrt(jnp.mean(of * of, axis=-1, keepdims=True) + NORM_EPS)


def modulate(h, shift, scale):
    return h * (1.0 + scale) + shift


def flip(a):
    return jnp.flip(a, axis=1)


def to_chunks(a, size):
    b, n = a.shape[:2]
    return jnp.moveaxis(a.reshape((b, n // size, size) + a.shape[2:]), 1, 0)


def from_chunks(a):
    nc, b, size = a.shape[:3]
    return jnp.moveaxis(a, 0, 1).reshape((b, nc * size) + a.shape[3:])


def rope_axis(xa, pos):
    half = xa.shape[-1] // 2
    inv = ROPE_BASE ** (-jnp.arange(half, dtype=jnp.float32) / half)
    ang = pos.astype(jnp.float32)[:, None] * inv[None, :]
    cos = jnp.cos(ang)[None, :, None, :]
    sin = jnp.sin(ang)[None, :, None, :]
    x1, x2 = xa[..., :half], xa[..., half:]
    return jnp.concatenate([x1 * cos - x2 * sin, x1 * sin + x2 * cos], axis=-1)


def axial_rope(x, row, col):
    d = x.shape[-1]
    xf = x.astype(jnp.float32)
    out = jnp.concatenate([rope_axis(xf[..., : d // 2], row), rope_axis(xf[..., d // 2:], col)], axis=-1)
    return out.astype(x.dtype)


def natten_mixer(hx, hc, w_qkv, rpb, w_o, need_ctx):
    b, n, d = hx.shape
    lc = hc.shape[1]
    rows = n // GRID_W
    kh = min(NA_KH, rows)
    scale = NA_HEAD_DIM ** -0.5
    qkv = (hx @ w_qkv).reshape(b, n, 3, NA_HEADS, NA_HEAD_DIM)
    q, k, v = qkv[:, :, 0] * scale, qkv[:, :, 1], qkv[:, :, 2]
    qkv_c = (hc @ w_qkv).reshape(b, lc, 3, NA_HEADS, NA_HEAD_DIM)
    qc, kc, vc = qkv_c[:, :, 0] * scale, qkv_c[:, :, 1], qkv_c[:, :, 2]

    qg = q.reshape(b, rows, GRID_W, NA_HEADS, NA_HEAD_DIM)
    kg = k.reshape(b, rows, GRID_W, NA_HEADS, NA_HEAD_DIM)
    vg = v.reshape(b, rows, GRID_W, NA_HEADS, NA_HEAD_DIM)

    qcol = jnp.arange(GRID_W)[:, None]
    kcol = jnp.arange(GRID_W)[None, :]
    cs = jnp.clip(qcol - NA_KW // 2, 0, GRID_W - NA_KW)
    col_ok = (kcol >= cs) & (kcol < cs + NA_KW)
    dc_idx = jnp.clip(kcol - qcol + NA_KW - 1, 0, 2 * NA_KW - 2)
    rpb_c = rpb.astype(jnp.float32)[:, :, dc_idx]
    mask = jnp.broadcast_to(col_ok[:, None, :], (GRID_W, kh, GRID_W)).reshape(GRID_W, kh * GRID_W)

    def row_block(r):
        rs = jnp.clip(r - kh // 2, 0, rows - kh)
        q_r = lax.dynamic_index_in_dim(qg, r, axis=1, keepdims=False)
        k_r = lax.dynamic_slice_in_dim(kg, rs, kh, axis=1).reshape(b, kh * GRID_W, NA_HEADS, NA_HEAD_DIM)
        v_r = lax.dynamic_slice_in_dim(vg, rs, kh, axis=1).reshape(b, kh * GRID_W, NA_HEADS, NA_HEAD_DIM)
        dr_idx = rs + jnp.arange(kh) - r + NA_KH - 1
        bias = jnp.take(rpb_c, dr_idx, axis=1).transpose(0, 2, 1, 3).reshape(NA_HEADS, GRID_W, kh * GRID_W)
        s_win = jnp.einsum('bqhd,bkhd->bhqk', q_r, k_r).astype(jnp.float32) + bias[None]
        s_win = jnp.where(mask[None, None], s_win, NEG_INF)
        s_ctx = jnp.einsum('bqhd,blhd->bhql', q_r, kc).astype(jnp.float32)
        p = jax.nn.softmax(jnp.concatenate([s_win, s_ctx], axis=-1), axis=-1).astype(v.dtype)
        return (jnp.einsum('bhqk,bkhd->bqhd', p[..., : kh * GRID_W], v_r)
                + jnp.einsum('bhql,blhd->bqhd', p[..., kh * GRID_W:], vc))

    o = lax.map(row_block, jnp.arange(rows))
    yx = jnp.moveaxis(o, 0, 1).reshape(b, n, d) @ w_o
    yc = None
    if need_ctx:
        s = jnp.einsum('bqhd,bkhd->bhqk', qc, kc).astype(jnp.float32)
        p = jax.nn.softmax(s, axis=-1).astype(vc.dtype)
        yc = jnp.einsum('bhqk,bkhd->bqhd', p, vc).reshape(b, lc, d) @ w_o
    return yx, yc


def retention_scan(q, k, v, log_gamma, s0):
    size = RET_CHUNK
    idx = jnp.arange(size, dtype=jnp.float32)
    diff = idx[:, None] - idx[None, :]
    intra = jnp.exp(jnp.where(diff[None] >= 0, diff[None] * log_gamma[:, None, None], -jnp.inf))
    q_dec = jnp.exp((idx + 1.0)[:, None] * log_gamma[None, :])
    k_dec = jnp.exp((size - 1.0 - idx)[:, None] * log_gamma[None, :])
    c_dec = jnp.exp(size * log_gamma)

    def step(s, blk):
        qb, kb, vb = blk
        sc = jnp.einsum('bihd,bjhd->bhij', qb, kb) * intra[None]
        o = (jnp.einsum('bhij,bjhe->bihe', sc, vb)
             + jnp.einsum('bihd,bhde->bihe', qb * q_dec[None, :, :, None], s))
        s = s * c_dec[None, :, None, None] + jnp.einsum('bjhd,bjhe->bhde', kb * k_dec[None, :, :, None], vb)
        return s, o

    f32 = jnp.float32
    s_fin, o = lax.scan(step, s0, (to_chunks(q.astype(f32), size), to_chunks(k.astype(f32), size),
                                   to_chunks(v.astype(f32), size)))
    return from_chunks(o), s_fin


def retention_mixer(hx, hc, w_in, decay_exp, w_o, row, col, need_ctx):
    b, n, d = hx.shape
    lc = hc.shape[1]

    def project(h, length):
        q, k, v, g = jnp.split(h @ w_in, 4, axis=-1)
        return (q.reshape(b, length, RET_HEADS, RET_DK), k.reshape(b, length, RET_HEADS, RET_DK),
                v.reshape(b, length, RET_HEADS, RET_DV), g)

    qx, kx, vx, gx = project(hx, n)
    qx = axial_rope(qx, row, col)
    kx = axial_rope(kx, row, col) * RET_DK ** -0.5
    qc, kc, vc, gc = project(hc, lc)
    kc = kc * RET_DK ** -0.5
    lg = jnp.log1p(-jnp.exp2(-decay_exp.astype(jnp.float32)))
    s0 = jnp.zeros((b, RET_HEADS, RET_DK, RET_DV), jnp.float32)
    oc_f, sc_f = retention_scan(qc, kc, vc, lg[0], s0)
    oc_b, sc_b = retention_scan(flip(qc), flip(kc), flip(vc), lg[1], s0)
    ox_f, _ = retention_scan(qx, kx, vx, lg[0], sc_f)
    ox_b, _ = retention_scan(flip(qx), flip(kx), flip(vx), lg[1], sc_b)

    def out(o, g):
        o = head_rms(o).reshape(o.shape[0], o.shape[1], d).astype(g.dtype)
        return (jax.nn.silu(g) * o) @ w_o

    yx = out(ox_f + flip(ox_b), gx)
    yc = out(oc_f + flip(oc_b), gc) if need_ctx else None
    return yx, yc


def mlstm_scan(q, k, v, ig, lf, state):
    size = ML_CHUNK
    tri = jnp.tril(jnp.ones((size, size), dtype=bool))[None, :, :, None]

    def step(carry, blk):
        c_mat, n_vec, m = carry
        qb, kb, vb, ib, fb = blk
        bsum = jnp.cumsum(fb, axis=1)
        log_w = jnp.where(tri, bsum[:, :, None, :] - bsum[:, None, :, :] + ib[:, None, :, :], -jnp.inf)
        log_p = bsum + m[:, None, :]
        m_i = jnp.maximum(log_p, jnp.max(log_w, axis=2))
        w = jnp.exp(log_w - m_i[:, :, None, :])
        w_p = jnp.exp(log_p - m_i)
        s = jnp.einsum('bihd,bjhd->bijh', qb, kb) * w
        num = (jnp.einsum('bijh,bjhe->bihe', s, vb)
               + w_p[..., None] * jnp.einsum('bihd,bhde->bihe', qb, c_mat))
        den = jnp.sum(s, axis=2) + w_p * jnp.einsum('bihd,bhd->bih', qb, n_vec)
        h = num / jnp.maximum(jnp.abs(den), jnp.exp(-m_i))[..., None]
        b_end = bsum[:, -1, :]
        log_e = b_end[:, None, :] - bsum + ib
        m_new = jnp.maximum(b_end + m, jnp.max(log_e, axis=1))
        w_e = jnp.exp(log_e - m_new[:, None, :])
        dec = jnp.exp(b_end + m - m_new)
        c_mat = dec[:, :, None, None] * c_mat + jnp.einsum('bjhd,bjhe->bhde', kb * w_e[..., None], vb)
        n_vec = dec[:, :, None] * n_vec + jnp.einsum('bjh,bjhd->bhd', w_e, kb)
        return (c_mat, n_vec, m_new), h

    f32 = jnp.float32
    xs = (to_chunks(q.astype(f32), size), to_chunks(k.astype(f32), size), to_chunks(v.astype(f32), size),
          to_chunks(ig.astype(f32), size), to_chunks(lf.astype(f32), size))
    st, h = lax.scan(step, state, xs)
    return from_chunks(h), st


def mlstm_mixer(hx, hc, w_in, w_gate, b_gate, norm_g, w_o, need_ctx):
    b, n, d = hx.shape
    lc = hc.shape[1]

    def project(h, length):
        p = h @ w_in
        q = p[..., :ML_QK_DIM].reshape(b, length, ML_HEADS, ML_DK)
        k = p[..., ML_QK_DIM:2 * ML_QK_DIM].reshape(b, length, ML_HEADS, ML_DK) * ML_DK ** -0.5
        v = p[..., 2 * ML_QK_DIM:2 * ML_QK_DIM + d].reshape(b, length, ML_HEADS, ML_DV)
        o = p[..., 2 * ML_QK_DIM + d:]
        gt = (h @ w_gate + b_gate).astype(jnp.float32).reshape(b, length, 4, ML_HEADS)
        return q, k, v, o, gt[:, :, 0], jax.nn.log_sigmoid(gt[:, :, 1]), gt[:, :, 2], jax.nn.log_sigmoid(gt[:, :, 3])

    qx, kx, vx, ox, ixf, fxf, ixb, fxb = project(hx, n)
    qc, kc, vc, oc, icf, fcf, icb, fcb = project(hc, lc)
    st0 = (jnp.zeros((b, ML_HEADS, ML_DK, ML_DV), jnp.float32),
           jnp.zeros((b, ML_HEADS, ML_DK), jnp.float32),
           jnp.zeros((b, ML_HEADS), jnp.float32))
    hc_f, st_f = mlstm_scan(qc, kc, vc, icf, fcf, st0)
    hc_b, st_b = mlstm_scan(flip(qc), flip(kc), flip(vc), flip(icb), flip(fcb), st0)
    hx_f, _ = mlstm_scan(qx, kx, vx, ixf, fxf, st_f)
    hx_b, _ = mlstm_scan(flip(qx), flip(kx), flip(vx), flip(ixb), flip(fxb), st_b)
    g = norm_g.astype(jnp.float32).reshape(ML_HEADS, ML_DV)

    def out(h, o):
        h = (head_rms(h) * g).reshape(h.shape[0], h.shape[1], d).astype(o.dtype)
        return (jax.nn.sigmoid(o) * h) @ w_o

    yx = out(hx_f + flip(hx_b), ox)
    yc = out(hc_f + flip(hc_b), oc) if need_ctx else None
    return yx, yc


def conv_ffn(h, w_up, conv_w, conv_b, w_down):
    u = h @ w_up
    u = lax.conv_general_dilated(u, conv_w[:, None, :].astype(u.dtype), window_strides=(1,),
                                 padding=((CONV_WIDTH // 2, CONV_WIDTH // 2),),
                                 dimension_numbers=('NWC', 'WIO', 'NWC'),
                                 feature_group_count=u.shape[-1]) + conv_b
    a, gt = jnp.split(u, 2, axis=-1)
    return (a * jax.nn.silu(gt)) @ w_down


def setup_inputs(seed: int = 0) -> dict:
    key = jax.random.key(seed)
    ks = iter(jax.random.split(key, 40))
    d = D_MODEL
    n_a = len(range(0, DEPTH, N_MIXERS))
    n_r = len(range(1, DEPTH, N_MIXERS))
    n_m = len(range(2, DEPTH, N_MIXERS))

    def nrm(shape, scale):
        return jax.random.normal(next(ks), shape, jnp.float32) * scale

    ret_base = jnp.linspace(RET_EXP_MIN, RET_EXP_MAX, RET_HEADS, dtype=jnp.float32)
    f_bias = jnp.linspace(ML_FGATE_BIAS_MIN, ML_FGATE_BIAS_MAX, ML_HEADS, dtype=jnp.float32)
    zeros_h = jnp.zeros((ML_HEADS,), jnp.float32)
    gate_base = jnp.stack([zeros_h, f_bias, zeros_h, f_bias])
    return {
        "x": nrm((BATCH, SEQ, d), 1.0),
        "c": nrm((BATCH, d), 1.0),
        "ctx": nrm((BATCH, CTX_LEN, d), 1.0),
        "c_ctx": nrm((d,), 1.0),
        "ada_w": nrm((DEPTH, d, 6 * d), 0.5 * d ** -0.5),
        "ada_b": nrm((DEPTH, 6 * d), 0.01),
        "norm1_g": 1.0 + nrm((DEPTH, d), 0.02),
        "norm2_g": 1.0 + nrm((DEPTH, d), 0.02),
        "na_w_qkv": nrm((n_a, d, 3 * d), d ** -0.5),
        "na_rpb": nrm((n_a, NA_HEADS, 2 * NA_KH - 1, 2 * NA_KW - 1), 0.02),
        "na_w_o": nrm((n_a, d, d), d ** -0.5),
        "ret_w_in": nrm((n_r, d, 4 * d), d ** -0.5),
        "ret_decay": jnp.broadcast_to(ret_base, (n_r, 2, RET_HEADS)) + nrm((n_r, 2, RET_HEADS), 0.05),
        "ret_w_o": nrm((n_r, d, d), d ** -0.5),
        "ml_w_in": nrm((n_m, d, 2 * ML_QK_DIM + 2 * d), d ** -0.5),
        "ml_w_gate": nrm((n_m, d, 4 * ML_HEADS), 0.1 * d ** -0.5),
        "ml_b_gate": (gate_base[None] + nrm((n_m, 4, ML_HEADS), 0.1)).reshape(n_m, 4 * ML_HEADS),
        "ml_norm_g": 1.0 + nrm((n_m, d), 0.02),
        "ml_w_o": nrm((n_m, d, d), d ** -0.5),
        "ffn_w_up": nrm((DEPTH, d, 2 * FFN_HIDDEN), d ** -0.5),
        "ffn_conv_w": nrm((DEPTH, CONV_WIDTH, 2 * FFN_HIDDEN), CONV_WIDTH ** -0.5),
        "ffn_conv_b": nrm((DEPTH, 2 * FFN_HIDDEN), 0.01),
        "ffn_w_down": nrm((DEPTH, FFN_HIDDEN, d), FFN_HIDDEN ** -0.5),
        "final_g": 1.0 + nrm((d,), 0.02),
    }


def reference(x, c, ctx, c_ctx, ada_w, ada_b, norm1_g, norm2_g, na_w_qkv, na_rpb, na_w_o,
              ret_w_in, ret_decay, ret_w_o, ml_w_in, ml_w_gate, ml_b_gate, ml_norm_g, ml_w_o,
              ffn_w_up, ffn_conv_w, ffn_conv_b, ffn_w_down, final_g):
    n = x.shape[1]
    pos = jnp.arange(n)
    row = pos // GRID_W
    col = pos % GRID_W
    silu_c = jax.nn.silu(c)
    silu_cc = jax.nn.silu(c_ctx)
    xc = ctx
    for i in range(DEPTH):
        last = i == DEPTH - 1
        j = i // N_MIXERS
        sh1, sc1, g1, sh2, sc2, g2 = jnp.split(silu_c @ ada_w[i] + ada_b[i], 6, axis=-1)
        sh1c, sc1c, g1c, sh2c, sc2c, g2c = jnp.split(silu_cc @ ada_w[i] + ada_b[i], 6, axis=-1)
        hx = modulate(rmsnorm(x, norm1_g[i]), sh1[:, None], sc1[:, None])
        hc = modulate(rmsnorm(xc, norm1_g[i]), sh1c, sc1c)
        kind = i % N_MIXERS
        if kind == 0:
            yx, yc = natten_mixer(hx, hc, na_w_qkv[j], na_rpb[j], na_w_o[j], not last)
        elif kind == 1:
            yx, yc = retention_mixer(hx, hc, ret_w_in[j], ret_decay[j], ret_w_o[j], row, col, not last)
        else:
            yx, yc = mlstm_mixer(hx, hc, ml_w_in[j], ml_w_gate[j], ml_b_gate[j], ml_norm_g[j], ml_w_o[j], not last)
        x = x + g1[:, None] * yx
        hx2 = modulate(rmsnorm(x, norm2_g[i]), sh2[:, None], sc2[:, None])
        x = x + g2[:, None] * conv_ffn(hx2, ffn_w_up[i], ffn_conv_w[i], ffn_conv_b[i], ffn_w_down[i])
        if not last:
            xc = xc + g1c * yc
            hc2 = modulate(rmsnorm(xc, norm2_g[i]), sh2c, sc2c)
            xc = xc + g2c * conv_ffn(hc2, ffn_w_up[i], ffn_conv_w[i], ffn_conv_b[i], ffn_w_down[i])
    return rmsnorm(x, final_g)
```

```python
import numpy as np
from contextlib import ExitStack
import concourse.bass as bass
import concourse.mybir as mybir
from concourse.bass_utils import run_bass_kernel_spmd

F32 = mybir.dt.float32
BF16 = mybir.dt.bfloat16
AF = mybir.ActivationFunctionType
ALU = mybir.AluOpType
AX = mybir.AxisListType

N_DMA_SEMS = 16
ENGS = ("pe", "dve", "act", "pool", "sp")
EPS = 1e-6


class Buf:
    __slots__ = ("name", "lw", "rd")

    def __init__(self, name=""):
        self.name = name
        self.lw = None
        self.rd = {}


class _Rec:
    def __getattr__(self, name):
        def f(*a, **kw):
            self.call = (name, a, kw)
            return self
        return f


class K:
    def __init__(self, nc, stack):
        self.nc = nc
        self.stack = stack
        self.q = {e: [] for e in ENGS}
        self.sem = {}
        self.cnt = {}
        for e in ("pe", "dve", "act", "pool", "cc"):
            self.sem[e] = stack.enter_context(nc.semaphore("s_" + e))
            self.cnt[e] = 0
        for i in range(N_DMA_SEMS):
            ch = "d%d" % i
            self.sem[ch] = stack.enter_context(nc.semaphore("s_" + ch))
            self.cnt[ch] = 0
        self.seen = {e: {} for e in ENGS}
        self.dma_rr = 0
        self.dma_rr_q = {}
        self.cc_names = set()
        self.n_ins = 0
        self.uid = 0
        self.scr = None

    def sb(self, name, shape, dt, stack=None):
        self.uid += 1
        return (stack or self.stack).enter_context(self.nc.sbuf_tensor("%s_%d" % (name, self.uid), shape, dt))

    def ps(self, name, shape, dt=F32, stack=None):
        self.uid += 1
        return (stack or self.stack).enter_context(self.nc.psum_tensor("%s_%d" % (name, self.uid), shape, dt))

    def dram(self, name, shape, dt, cc=True):
        if cc:
            self.cc_names.add(name)
        return self.nc.dram_tensor(name, list(shape), dt)

    def _deps(self, reads, writes):
        deps = {}
        for b in reads:
            if b.lw is not None:
                c, v = b.lw
                if deps.get(c, 0) < v:
                    deps[c] = v
        for b in writes:
            if b.lw is not None:
                c, v = b.lw
                if deps.get(c, 0) < v:
                    deps[c] = v
            for c, v in b.rd.items():
                if deps.get(c, 0) < v:
                    deps[c] = v
        return deps

    def _emit_waits(self, eng, deps):
        seen = self.seen[eng]
        for c, v in deps.items():
            if c == eng and eng == "pe":
                continue
            if seen.get(c, 0) >= v:
                continue
            seen[c] = v
            sem = self.sem[c]
            self.q[eng].append(lambda e, sem=sem, v=v: e.wait_ge(sem, v))
            if c == "pe" and eng in ("dve", "act", "pool") and self.scr is not None:
                scr = self.scr
                for _ in range(2):
                    if eng == "act":
                        self.q[eng].append(lambda e, scr=scr: e.activation(out=scr[0:1, 2:4], in_=scr[0:1, 4:6], func=AF.Copy))
                    elif eng == "dve":
                        self.q[eng].append(lambda e, scr=scr: e.memset(scr[0:1, 0:2], 0.0))
                    else:
                        self.q[eng].append(lambda e, scr=scr: e.memset(scr[0:1, 6:8], 0.0))

    def _mark(self, chan, val, reads, writes):
        for b in reads:
            if b.rd.get(chan, 0) < val:
                b.rd[chan] = val
        for b in writes:
            b.lw = (chan, val)
            b.rd = {}

    def op(self, eng, fn, reads=(), writes=()):
        deps = self._deps(reads, writes)
        self._emit_waits(eng, deps)
        self.cnt[eng] += 1
        val = self.cnt[eng]
        sem = self.sem[eng]
        rec = _Rec()
        fn(rec)
        name, a, kw = rec.call
        self.q[eng].append(lambda e, name=name, a=a, kw=kw, sem=sem: getattr(e, name)(*a, **kw).then_inc(sem, 1))
        self._mark(eng, val, reads, writes)
        self.n_ins += 1

    def dma(self, out, in_, reads=(), writes=(), queue="sp", **kw):
        deps = self._deps(reads, writes)
        half = N_DMA_SEMS // 2
        rr = self.dma_rr_q.get(queue, 0)
        self.dma_rr_q[queue] = (rr + 1) % half
        ch = "d%d" % (rr + (0 if queue == "sp" else half))
        if self.cnt[ch] > 0:
            deps[ch] = max(deps.get(ch, 0), self.cnt[ch])
        self._emit_waits(queue, deps)
        self.cnt[ch] += 16
        val = self.cnt[ch]
        sem = self.sem[ch]
        self.q[queue].append(
            lambda e, out=out, in_=in_, sem=sem, kw=kw: e.dma_start(out=out, in_=in_, **kw).then_inc(sem, 16))
        self._mark(ch, val, reads, writes)
        self.n_ins += 1

    def allgather(self, in_t, out_t, ncores, reads=(), writes=()):
        deps = self._deps(reads, writes)
        self._emit_waits("pool", deps)
        self.cnt["cc"] += 1
        val = self.cnt["cc"]
        sem = self.sem["cc"]
        self.q["pool"].append(
            lambda e: e.collective_compute("AllGather", ALU.bypass, replica_groups=[list(range(ncores))],
                                           ins=[in_t.ap().opt()], outs=[out_t.ap().opt()]).then_inc(sem, 1))
        self._mark("cc", val, reads, writes)

    def barrier(self):
        deps = {c: v for c, v in self.cnt.items() if v > 0}
        for e in ENGS:
            self._emit_waits(e, dict(deps))

    def finish(self):
        self.barrier()
        nc = self.nc
        q = self.q
        with nc.Block() as block:
            @block.tensor
            def _(e):
                for f in q["pe"]:
                    f(e)

            @block.vector
            def _(e):
                for f in q["dve"]:
                    f(e)

            @block.scalar
            def _(e):
                for f in q["act"]:
                    f(e)

            @block.gpsimd
            def _(e):
                for f in q["pool"]:
                    f(e)

            @block.sync
            def _(e):
                for f in q["sp"]:
                    f(e)


def make_cfg(D=4096, SEQ=8192, CTX=256, DEPTH=4, NA_HEADS=32, RET_HEADS=16, ML_HEADS=8, ML_QK=None):
    c = dict(D=D, SEQ=SEQ, CTX=CTX, L=DEPTH, T=CTX + SEQ, KC=D // 128, GW=64)
    c["NA_HEADS"] = NA_HEADS
    c["RET_HEADS"] = RET_HEADS
    c["RET_DK"] = D // RET_HEADS
    c["ML_HEADS"] = ML_HEADS
    c["ML_QK"] = ML_QK if ML_QK is not None else D // 2
    c["ML_DK"] = c["ML_QK"] // ML_HEADS
    c["ML_DV"] = D // ML_HEADS
    assert CTX == 256 and SEQ % 512 == 0
    return c


def tok_blocks(cfg, bs=512):
    return [(0, cfg["CTX"])] + [(cfg["CTX"] + i * bs, bs) for i in range(cfg["SEQ"] // bs)]


DEBUG_DUMPS = False


class P:
    def __init__(self, nc, k, cfg):
        self.nc, self.k, self.cfg = nc, k, cfg
        self.inputs = {}
        self.dbg = {}

    def inp(self, name, shape, dt=F32):
        t = self.nc.dram_tensor(name, list(shape), dt, kind="ExternalInput")
        self.inputs[name] = t
        return t


def build_consts(p):
    k = p.k
    c = {}
    c["ident"] = k.sb("ident", [128, 128], F32)
    c["identb"] = k.sb("identb", [128, 128], BF16)
    c["onesb"] = k.sb("onesb", [128, 128], BF16)
    c["onesf"] = k.sb("onesf", [128, 128], F32)
    c["triu"] = k.sb("triu", [128, 128], F32)
    c["tril"] = k.sb("tril", [128, 128], F32)
    c["epsc"] = k.sb("epsc", [128, 1], F32)
    scr = k.sb("scr", [128, 8], F32)
    b = Buf("consts")
    k.op("pool", lambda e: e.memset(scr[:], 0.0), writes=[b])
    k.scr = None
    c["buf"] = b
    k.op("pool", lambda e: e.memset(c["ident"][:], 1.0), writes=[b])
    k.op("pool", lambda e: e.affine_select(out=c["ident"][:], in_=c["ident"][:], pattern=[[1, 128]],
                                           compare_op=ALU.is_equal, fill=0.0, base=0, channel_multiplier=-1),
         reads=[b], writes=[b])
    k.op("pool", lambda e: e.tensor_copy(out=c["identb"][:], in_=c["ident"][:]), reads=[b], writes=[b])
    k.op("pool", lambda e: e.memset(c["onesb"][:], 1.0), writes=[b])
    k.op("pool", lambda e: e.memset(c["onesf"][:], 1.0), writes=[b])
    k.op("pool", lambda e: e.memset(c["epsc"][:], EPS), writes=[b])
    k.op("pool", lambda e: e.affine_select(out=c["triu"][:], in_=c["onesf"][:], pattern=[[1, 128]],
                                           compare_op=ALU.is_ge, fill=0.0, base=0, channel_multiplier=-1), reads=[b], writes=[b])
    k.op("pool", lambda e: e.affine_select(out=c["tril"][:], in_=c["onesf"][:], pattern=[[-1, 128]],
                                           compare_op=ALU.is_ge, fill=0.0, base=0, channel_multiplier=1), reads=[b], writes=[b])
    p.c = c


def phase_adaln(p):
    k, cfg = p.k, p.cfg
    L, KC, D = cfg["L"], cfg["KC"], cfg["D"]
    NCOL = 6 * D
    cvec = p.inp("cvec", [128, KC, 2])
    adaw = p.inp("adaw", [L, D, NCOL])
    adab = p.inp("adab", [L, 1, NCOL])
    p.mod = k.dram("mod", [L * 2 * 6, D], F32)
    p.b_mod = Buf("mod")
    with ExitStack() as st:
        cv = k.sb("cv", [128, KC, 2], F32, st)
        sg = k.sb("sg", [128, KC, 2], F32, st)
        bcv = Buf()
        k.dma(cv[:], cvec.ap(), writes=[bcv])
        k.op("act", lambda e: e.activation(out=sg[:], in_=cv[:], func=AF.Sigmoid), reads=[bcv], writes=[bcv])
        k.op("dve", lambda e: e.tensor_tensor(out=cv[:], in0=cv[:], in1=sg[:], op=ALU.mult), reads=[bcv], writes=[bcv])
        CW = 512
        KG = 8
        NWB = 4
        wt = [k.sb("adw%d" % i, [128, KG, CW], F32, st) for i in range(NWB)]
        bwt = [Buf() for _ in range(NWB)]
        pt = [k.ps("adp%d" % i, [128, 512], F32, st) for i in range(2)]
        bpt = [Buf() for _ in range(2)]
        ob = [k.sb("ado%d" % i, [2, CW], F32, st) for i in range(2)]
        bob = [Buf() for _ in range(2)]
        bb = [k.sb("adb%d" % i, [2, CW], F32, st) for i in range(2)]
        bbb = [Buf() for _ in range(2)]
        it = 0
        widx = 0
        for l in range(L):
            for cb in range(NCOL // CW):
                i2 = it % 2
                it += 1
                for kg in range(KC // KG):
                    w2 = widx % NWB
                    widx += 1
                    src = adaw.ap()[l, kg * KG * 128:(kg + 1) * KG * 128, cb * CW:(cb + 1) * CW].rearrange(
                        "(c p) n -> p c n", p=128)
                    k.dma(wt[w2][:], src, writes=[bwt[w2]])
                    for cc in range(KG):
                        ci = kg * KG + cc
                        k.op("pe", lambda e, w2=w2, cc=cc, ci=ci, i2=i2: e.matmul(
                            pt[i2][0:2, 0:CW], lhsT=cv[:, ci, :], rhs=wt[w2][:, cc, :],
                            start=(ci == 0), stop=(ci == KC - 1)),
                            reads=[bcv, bwt[w2]], writes=[bpt[i2]])
                for v in range(2):
                    k.dma(bb[i2][v:v + 1, :], adab.ap()[l, :, cb * CW:(cb + 1) * CW], writes=[bbb[i2]])
                k.op("dve", lambda e, i2=i2: e.tensor_tensor(out=ob[i2][:], in0=pt[i2][0:2, 0:CW], in1=bb[i2][:], op=ALU.add),
                     reads=[bpt[i2], bbb[i2]], writes=[bob[i2]])
                piece = (cb * CW) // D
                off = (cb * CW) % D
                for v in range(2):
                    row = (l * 2 + v) * 6 + piece
                    k.dma(p.mod.ap()[row:row + 1, off:off + CW], ob[i2][v:v + 1, :],
                          reads=[bob[i2]], writes=[p.b_mod], queue="pool")
    k.barrier()


def load_vecT(p, st, src, n, name):
    k = p.k
    res = k.sb(name, [128, n], F32, st)
    bres = Buf(name)
    if getattr(p, "_vt_stack", None) is not st:
        p._vt_stack = st
        p._vt_tmp = [k.sb("vtm_tmp%d" % i, [128, 128], F32, st) for i in range(2)]
        p._vt_btmp = [Buf() for _ in range(2)]
        p._vt_ps = [k.ps("vtm_ps%d" % i, [128, 512], F32, st) for i in range(2)]
        p._vt_bps = [Buf() for _ in range(2)]
        p._vt_i = 0
    i2 = p._vt_i % 2
    p._vt_i += 1
    tmp, btmp, ptile, bps = p._vt_tmp[i2], p._vt_btmp[i2], p._vt_ps[i2], p._vt_bps[i2]
    k.dma(tmp[0:n, :], src, reads=[p.b_mod], writes=[btmp])
    k.op("pe", lambda e: e.transpose(ptile[:, 0:n], tmp[0:n, :], p.c["ident"][0:n, 0:n]),
         reads=[btmp, p.c["buf"]], writes=[bps])
    k.op("dve", lambda e: e.tensor_copy(out=res[:, 0:n], in_=ptile[:, 0:n]), reads=[bps], writes=[bres])
    return res, bres


def layer_vectors(p, l, pst):
    k, cfg = p.k, p.cfg
    KC = cfg["KC"]
    p.b_vec = Buf("vec")
    p.b_bias = Buf("bias")
    out = {}
    pers = []
    for which in range(2):
        pers.append((k.sb("gmod%d_%d" % (which, l), [128, 2, KC], F32, pst),
                     k.sb("shb%d_%d" % (which, l), [128, KC, 2], BF16, pst),
                     k.sb("gate%d_%d" % (which, l), [128, KC, 2], F32, pst)))
    st = ExitStack()
    for which, (pc_sh, pc_sc, pc_g, ngname) in enumerate([(0, 1, 2, "n1g"), (3, 4, 5, "n2g")]):
        gmod, shb, gate = pers[which]
        gt = k.sb("ng%d_%d" % (which, l), [128, KC], F32, st)
        bgt = Buf()
        k.dma(gt[:], p.inputs[ngname].ap()[l], writes=[bgt])
        for v in range(2):
            def vec(piece, nm):
                a = (l * 2 + v) * 6 + piece
                return load_vecT(p, st, p.mod.ap()[a:a + 1, :].rearrange("o (c q) -> (o c) q", q=128), KC, nm)
            sc, bsc = vec(pc_sc, "sc")
            sh, bsh = vec(pc_sh, "sh")
            g, bg = vec(pc_g, "g")
            k.op("dve", lambda e, sc=sc, v=v, gmod=gmod, gt=gt: e.scalar_tensor_tensor(
                out=gmod[:, v, :], in0=sc[:, 0:KC], scalar=1.0, in1=gt[:], op0=ALU.add, op1=ALU.mult),
                reads=[bsc, bgt], writes=[p.b_vec])
            k.op("dve", lambda e, sh=sh, v=v, shb=shb: e.tensor_copy(out=shb[:, :, v], in_=sh[:, 0:KC]),
                 reads=[bsh], writes=[p.b_vec])
            k.op("dve", lambda e, g=g, v=v, gate=gate: e.tensor_copy(out=gate[:, :, v], in_=g[:, 0:KC]),
                 reads=[bg], writes=[p.b_vec])
        out["gmod%d" % (which + 1)] = gmod
        out["shb%d" % (which + 1)] = shb
        out["gate%d" % (which + 1)] = gate
    k.barrier()
    st.close()
    return out


def phase_prep(p, slab, bslab, stats, bstats, hT=None, bhT=None, kscale=None):
    k, cfg = p.k, p.cfg
    KC, T, CTX = cfg["KC"], cfg["T"], cfg["CTX"]
    PB = 256
    k.barrier()
    with ExitStack() as st:
        xf = [k.sb("pp_x%d" % i, [128, KC, PB], F32, st) for i in range(2)]
        bx = [Buf() for _ in range(2)]
        sq = [k.sb("pp_q%d" % i, [128, KC, PB], BF16, st) for i in range(2)]
        bq = [Buf() for _ in range(2)]
        hb = [k.sb("pp_h%d" % i, [128, KC, PB], BF16, st) for i in range(2)]
        bh = [Buf() for _ in range(2)]
        pt = [k.ps("pp_p%d" % i, [128, 512], F32, st) for i in range(2)]
        bp = [Buf() for _ in range(2)]
        ot = [k.sb("pp_o%d" % i, [1, PB], F32, st) for i in range(2)]
        bo = [Buf() for _ in range(2)]
        for bi in range(T // PB):
            i2 = bi % 2
            t0 = bi * PB
            k.dma(xf[i2][:], slab.ap()[:, t0:t0 + PB].rearrange("(c p) t -> p c t", p=128), reads=[bslab], writes=[bx[i2]])
            k.op("act", lambda e, i2=i2: e.activation(out=sq[i2][:], in_=xf[i2][:], func=AF.Square), reads=[bx[i2]], writes=[bq[i2]])
            for c in range(KC):
                k.op("pe", lambda e, i2=i2, c=c: e.matmul(pt[i2][0:1, 0:PB], lhsT=p.c["onesb"][:, 0:1], rhs=sq[i2][:, c, :],
                                                          start=(c == 0), stop=(c == KC - 1)),
                     reads=[bq[i2], p.c["buf"]], writes=[bp[i2]])
            k.op("dve", lambda e, i2=i2: e.tensor_copy(out=ot[i2][:], in_=pt[i2][0:1, 0:PB]), reads=[bp[i2]], writes=[bo[i2]])
            k.dma(stats.ap()[0:1, t0:t0 + PB], ot[i2][:], reads=[bo[i2]], writes=[bstats], queue="pool")
            if hT is not None:
                sc = kscale[:, 1 if t0 < CTX else 0, :]
                half = KC // 2
                k.op("dve", lambda e, i2=i2, sc=sc: e.tensor_tensor(
                    out=hb[i2][:, 0:half, :], in0=xf[i2][:, 0:half, :], in1=sc[:, 0:half].unsqueeze(2).to_broadcast([128, half, PB]), op=ALU.mult),
                    reads=[bx[i2], p.b_vec], writes=[bh[i2]])
                k.op("pool", lambda e, i2=i2, sc=sc: e.tensor_tensor(
                    out=hb[i2][:, half:KC, :], in0=xf[i2][:, half:KC, :], in1=sc[:, half:KC].unsqueeze(2).to_broadcast([128, KC - half, PB]), op=ALU.mult),
                    reads=[bx[i2], p.b_vec], writes=[bh[i2]])
                k.dma(hT.ap()[:, t0:t0 + PB].rearrange("(c p) t -> p c t", p=128), hb[i2][:], reads=[bh[i2]], writes=[bhT], queue="pool")
    k.barrier()


def make_rstd_cache(p, st, stats, bstats, nfeat):
    k, cfg = p.k, p.cfg
    T = cfg["T"]
    rs = k.sb("rstd_all", [128, T], F32, st)
    brs = Buf("rstd")
    with ExitStack() as st2:
        pt = [k.ps("rs_p%d" % i, [128, 512], F32, st2) for i in range(2)]
        bpt = [Buf() for _ in range(2)]
        tmp = [k.sb("rs_t%d" % i, [1, 512], F32, st2) for i in range(2)]
        btmp = [Buf() for _ in range(2)]
        for bi, (t0, n) in enumerate(tok_blocks(cfg)):
            i2 = bi % 2
            k.dma(tmp[i2][0:1, 0:n], stats.ap()[0:1, t0:t0 + n], reads=[bstats], writes=[btmp[i2]])
            k.op("pe", lambda e, i2=i2, n=n: e.matmul(pt[i2][:, 0:n], lhsT=p.c["onesf"][0:1, :], rhs=tmp[i2][0:1, 0:n], start=True, stop=True),
                 reads=[btmp[i2], p.c["buf"]], writes=[bpt[i2]])
            k.op("act", lambda e, i2=i2, n=n, t0=t0: e.activation(out=rs[:, t0:t0 + n], in_=pt[i2][:, 0:n], func=AF.Sqrt,
                                                                   bias=p.c["epsc"][:], scale=1.0 / nfeat),
                 reads=[bpt[i2], p.c["buf"]], writes=[brs])
            k.op("dve", lambda e, n=n, t0=t0: e.reciprocal(out=rs[:, t0:t0 + n], in_=rs[:, t0:t0 + n]), reads=[brs], writes=[brs])
        k.barrier()
    return rs, brs


def phase_gemm(p, w_ap, ncols, xin, bxin, epi, bias_vecs=None, bias_out=None, tb_hook=None, maxm=8):
    k, cfg = p.k, p.cfg
    T, CTX = cfg["T"], cfg["CTX"]
    KCin = w_ap.shape[0] // 128
    MBLK = (ncols + 127) // 128
    MAXM = maxm
    blocks = tok_blocks(cfg)
    for m0 in range(0, MBLK, MAXM):
        mcnt = min(MAXM, MBLK - m0)
        cw = min(ncols - m0 * 128, mcnt * 128)
        k.barrier()
        with ExitStack() as st:
            wbf = k.sb("g_wbf", [128, KCin, mcnt * 128], BF16, st)
            bw = Buf()
            with ExitStack() as st2:
                KG = 4
                stg = [k.sb("g_wst%d" % i, [128, KG, mcnt * 128], F32, st2) for i in range(2)]
                bstg = [Buf() for _ in range(2)]
                for kg in range(KCin // KG):
                    i2 = kg % 2
                    src = w_ap[kg * KG * 128:(kg + 1) * KG * 128, m0 * 128:m0 * 128 + cw].rearrange("(c p) n -> p c n", p=128)
                    k.dma(stg[i2][:, :, 0:cw], src, writes=[bstg[i2]])
                    eng = "dve" if kg % 2 == 0 else "pool"
                    k.op(eng, lambda e, i2=i2, kg=kg: e.tensor_copy(out=wbf[:, kg * KG:(kg + 1) * KG, 0:cw], in_=stg[i2][:, :, 0:cw]),
                         reads=[bstg[i2]], writes=[bw])
                k.barrier()
            pts = [k.ps("g_ps%d" % i, [128, 512], F32, st) for i in range(4)]
            bpts = [Buf() for _ in range(4)]
            pidx = 0
            if bias_vecs is not None:
                for mb in range(mcnt):
                    mw = min(128, cw - mb * 128)
                    pi = pidx % 4
                    pidx += 1
                    for c in range(KCin):
                        k.op("pe", lambda e, mb=mb, c=c, pi=pi, mw=mw: e.matmul(
                            pts[pi][0:mw, 0:2], lhsT=wbf[:, c, mb * 128:mb * 128 + mw], rhs=bias_vecs[:, c, :],
                            start=(c == 0), stop=(c == KCin - 1)), reads=[bw, p.b_vec], writes=[bpts[pi]])
                    k.op("dve", lambda e, mb=mb, pi=pi, mw=mw: e.tensor_copy(out=bias_out[0:mw, m0 + mb, :], in_=pts[pi][0:mw, 0:2]),
                         reads=[bpts[pi]], writes=[p.b_bias])
            xb = [k.sb("g_xb%d" % i, [128, KCin, 512], BF16, st) for i in range(2)]
            bxb = [Buf() for _ in range(2)]
            for bi, (t0, n) in enumerate(blocks):
                i2 = bi % 2
                k.dma(xb[i2][:, :, 0:n], xin.ap()[:, t0:t0 + n].rearrange("(c p) t -> p c t", p=128), reads=[bxin], writes=[bxb[i2]])
                if tb_hook is not None:
                    tb_hook(t0, n)
                for mb in range(mcnt):
                    mw = min(128, cw - mb * 128)
                    pi = pidx % 4
                    pidx += 1
                    for c in range(KCin):
                        k.op("pe", lambda e, mb=mb, c=c, pi=pi, i2=i2, n=n, mw=mw: e.matmul(
                            pts[pi][0:mw, 0:n], lhsT=wbf[:, c, mb * 128:mb * 128 + mw], rhs=xb[i2][:, c, 0:n],
                            start=(c == 0), stop=(c == KCin - 1)), reads=[bw, bxb[i2]], writes=[bpts[pi]])
                    epi(m0 + mb, mw, t0, n, pts[pi], bpts[pi])
    k.barrier()


def epi_norm_factory(p, st, rs, brs, bias, dst_fn, row_scale=None, tag="en", rope=None):
    k, cfg = p.k, p.cfg
    CTX = cfg["CTX"]
    ob = [k.sb("%s_ob%d" % (tag, i), [128, 512], BF16, st) for i in range(3)]
    of = [k.sb("%s_of%d" % (tag, i), [128, 512], F32, st) for i in range(3)]
    tt = [k.sb("%s_t%d" % (tag, i), [128, 512], F32, st) for i in range(3)]
    bo = [Buf() for _ in range(3)]
    state = {"i": 0}

    def epi(mb, mw, t0, n, pt, bpt):
        i3 = state["i"] % 3
        state["i"] += 1
        cls = 1 if t0 < CTX else 0
        dst, bdst, row0, is_bf = dst_fn(mb)
        s = 1.0 if row_scale is None else row_scale(mb)
        rc = rope["is_rope"](mb) if (rope is not None and t0 >= CTX) else None
        k.op("dve", lambda e: e.tensor_tensor(out=tt[i3][0:mw, 0:n], in0=pt[0:mw, 0:n], in1=rs[0:mw, t0:t0 + n], op=ALU.mult),
             reads=[bpt, brs], writes=[bo[i3]])
        o = ob[i3] if (is_bf and rc is None) else of[i3]
        k.op("dve", lambda e: e.tensor_scalar(out=o[0:mw, 0:n], in0=tt[i3][0:mw, 0:n], scalar1=bias[0:mw, mb, cls:cls + 1],
                                              scalar2=float(s), op0=ALU.add, op1=ALU.mult),
             reads=[bo[i3], p.b_bias], writes=[bo[i3]])
        if rc is not None:
            rope["apply"](rc, t0, n, of[i3], ob[i3], tt[i3], bo[i3])
            o = ob[i3]
        k.dma(dst.ap()[row0:row0 + mw, t0:t0 + n], o[0:mw, 0:n], reads=[bo[i3]], writes=[bdst], queue="pool")
    return epi


def epi_resid_factory(p, st, gate, old_slab, bold, new_slab, bnew, tag="er"):
    k, cfg = p.k, p.cfg
    CTX = cfg["CTX"]
    xo = [k.sb("%s_x%d" % (tag, i), [128, 512], F32, st) for i in range(3)]
    bx = [Buf() for _ in range(3)]
    state = {"i": 0}

    def epi(mb, mw, t0, n, pt, bpt):
        i3 = state["i"] % 3
        state["i"] += 1
        cls = 1 if t0 < CTX else 0
        k.dma(xo[i3][:, 0:n], old_slab.ap()[mb * 128:(mb + 1) * 128, t0:t0 + n], reads=[bold], writes=[bx[i3]])
        k.op("dve", lambda e: e.scalar_tensor_tensor(out=xo[i3][:, 0:n], in0=pt[:, 0:n], scalar=gate[:, mb, cls:cls + 1],
                                                     in1=xo[i3][:, 0:n], op0=ALU.mult, op1=ALU.add),
             reads=[bpt, bx[i3], p.b_vec], writes=[bx[i3]])
        k.dma(new_slab.ap()[mb * 128:(mb + 1) * 128, t0:t0 + n], xo[i3][:, 0:n], reads=[bx[i3]], writes=[bnew], queue="pool")
    return epi
def phase_convgate(p, l, u, bu, f_loc, bf):
    k, cfg = p.k, p.cfg
    FPC, T, MB, CTX = cfg["D"], cfg["T"], cfg["KC"], cfg["CTX"]
    cw = p.inputs["convw"]
    cb = p.inputs["convb"]
    SEG = 1024
    with ExitStack() as st:
        cwt = k.sb("cwt", [128, 2 * MB, 3], F32, st)
        cbt = k.sb("cbt", [128, 2 * MB], F32, st)
        bc = Buf()
        k.dma(cwt[:], cw.ap()[l], writes=[bc])
        k.dma(cbt[:], cb.ap()[l], writes=[bc])
        ub = [[k.sb("cv_u%d%d" % (i, j), [128, SEG + 2], F32, st) for j in range(2)] for i in range(2)]
        bub = [[Buf() for j in range(2)] for i in range(2)]
        acc = [[k.sb("cv_a%d%d" % (i, j), [128, SEG], F32, st) for j in range(2)] for i in range(2)]
        bacc = [[Buf() for j in range(2)] for i in range(2)]
        fo = [k.sb("cv_f%d" % i, [128, SEG], BF16, st) for i in range(2)]
        bfo = [Buf() for _ in range(2)]
        it = 0
        segs = [(0, CTX, 0, CTX)]
        for s0 in range(CTX, T, SEG):
            segs.append((s0, min(SEG, T - s0), CTX, T))
        for mb in range(MB):
            for (s0, n, lo, hi) in segs:
                i2 = it % 2
                it += 1
                for j in range(2):
                    row0 = (j * MB + mb) * 128
                    ut = ub[i2][j]
                    a0 = max(s0 - 1, lo)
                    a1 = min(s0 + n + 1, hi)
                    if a0 == s0:
                        k.op("pool", lambda e, ut=ut: e.memset(ut[:, 0:1], 0.0), writes=[bub[i2][j]])
                    if a1 == s0 + n:
                        k.op("pool", lambda e, ut=ut, n=n: e.memset(ut[:, n + 1:n + 2], 0.0), writes=[bub[i2][j]])
                    k.dma(ut[:, (a0 - s0 + 1):(a1 - s0 + 1)], u[j].ap()[mb * 128:(mb + 1) * 128, a0:a1], reads=[bu], writes=[bub[i2][j]])
                    at = acc[i2][j]
                    ci = j * MB + mb
                    k.op("dve", lambda e, ut=ut, at=at, ci=ci, n=n: e.tensor_scalar(
                        out=at[:, 0:n], in0=ut[:, 0:n], scalar1=cwt[:, ci, 0:1], scalar2=cbt[:, ci:ci + 1], op0=ALU.mult, op1=ALU.add),
                        reads=[bub[i2][j], bc], writes=[bacc[i2][j]])
                    for tap in (1, 2):
                        k.op("dve", lambda e, ut=ut, at=at, ci=ci, n=n, tap=tap: e.scalar_tensor_tensor(
                            out=at[:, 0:n], in0=ut[:, tap:tap + n], scalar=cwt[:, ci, tap:tap + 1], in1=at[:, 0:n],
                            op0=ALU.mult, op1=ALU.add), reads=[bub[i2][j], bc, bacc[i2][j]], writes=[bacc[i2][j]])
                sg = ub[i2][1]
                k.op("act", lambda e, sg=sg, n=n, i2=i2: e.activation(out=sg[:, 0:n], in_=acc[i2][1][:, 0:n], func=AF.Sigmoid),
                     reads=[bacc[i2][1], bub[i2][1]], writes=[bub[i2][1]])
                k.op("dve", lambda e, sg=sg, n=n, i2=i2: e.tensor_tensor(out=acc[i2][1][:, 0:n], in0=acc[i2][1][:, 0:n], in1=sg[:, 0:n], op=ALU.mult),
                     reads=[bub[i2][1], bacc[i2][1]], writes=[bacc[i2][1]])
                k.op("dve", lambda e, n=n, i2=i2: e.tensor_tensor(out=fo[i2][:, 0:n], in0=acc[i2][0][:, 0:n], in1=acc[i2][1][:, 0:n], op=ALU.mult),
                     reads=[bacc[i2][0], bacc[i2][1]], writes=[bfo[i2]])
                k.dma(f_loc.ap()[mb * 128:(mb + 1) * 128, s0:s0 + n], fo[i2][:, 0:n], reads=[bfo[i2]], writes=[bf], queue="pool")
    k.barrier()


def phase_natten(p, jn, qkvT, bqkv, o_loc, bo, need_ctx):
    k, cfg = p.k, p.cfg
    FPC, T, CTX, SEQ = cfg["D"], cfg["T"], cfg["CTX"], cfg["SEQ"]
    HPC = FPC // 128
    R = SEQ // 64
    KH = min(8, R)
    WK = KH * 64
    rpbT = p.inputs["rpbT"]
    namask = p.inputs["namask"]
    NB64 = T // 64
    identb = p.c["identb"]
    k.barrier()
    with ExitStack() as st:
        qT = k.sb("na_q", [128, T], BF16, st)
        kT = k.sb("na_k", [128, T], BF16, st)
        vT = k.sb("na_v", [128, T], BF16, st)
        v64 = k.sb("na_v64", [64, NB64, 128], BF16, st)
        oT = k.sb("na_o", [128, T], BF16, st)
        Bh = k.sb("na_B", [64, 15, 64], F32, st)
        mk = k.sb("na_m", [64, 64], F32, st)
        bq, bk, bv, bv64, boT, bB, bm = [Buf() for _ in range(7)]
        k.dma(mk[:], namask.ap(), writes=[bm])
        S_ps = [k.ps("na_S%d" % i, [64, 1024], F32, st) for i in range(2)]
        PT_ps = [k.ps("na_PT%d" % i, [64, 1024], BF16, st) for i in range(2)]
        O_ps = [k.ps("na_O%d" % i, [128, 512], F32, st) for i in range(2)]
        bS, bPT, bO = [[Buf() for _ in range(2)] for _ in range(3)]
        NKM = WK + CTX
        Ssb = [k.sb("na_Ss%d" % i, [64, NKM], F32, st) for i in range(2)]
        Pn = [k.sb("na_Pn%d" % i, [64, NKM], BF16, st) for i in range(2)]
        PTs = [k.sb("na_PTs%d" % i, [64, NKM], BF16, st) for i in range(2)]
        st1 = [k.sb("na_st%d" % i, [64, 4], F32, st) for i in range(2)]
        bSs, bPn, bPTs, bst = [[Buf() for _ in range(2)] for _ in range(4)]
        it = 0
        for h in range(HPC):
            k.dma(qT[:], qkvT.ap()[h * 128:(h + 1) * 128, :], reads=[bqkv], writes=[bq])
            k.dma(kT[:], qkvT.ap()[FPC + h * 128:FPC + (h + 1) * 128, :], reads=[bqkv], writes=[bk])
            k.dma(vT[:], qkvT.ap()[2 * FPC + h * 128:2 * FPC + (h + 1) * 128, :], reads=[bqkv], writes=[bv])
            k.dma(Bh[:], rpbT.ap()[jn, h].rearrange("q (a c) -> q a c", c=64), writes=[bB])
            k.op("dve", lambda e: e.tensor_tensor(out=Bh[:], in0=Bh[:], in1=mk[:].unsqueeze(1).to_broadcast([64, 15, 64]), op=ALU.add),
                 reads=[bB, bm], writes=[bB])
            if not need_ctx:
                k.op("pool", lambda e: e.memset(oT[:, 0:CTX], 0.0), writes=[boT])
            for g in range(0, NB64, 8):
                i2 = it % 2
                it += 1
                nb = min(8, NB64 - g)
                for b in range(nb):
                    k.op("pe", lambda e, g=g, b=b, i2=i2: e.transpose(PT_ps[i2][0:64, b * 128:(b + 1) * 128],
                                                                       vT[:, (g + b) * 64:(g + b + 1) * 64], identb[:]),
                         reads=[bv, p.c["buf"]], writes=[bPT[i2]])
                k.op("act", lambda e, g=g, nb=nb, i2=i2: e.activation(
                    out=v64[:, g:g + nb, :], in_=PT_ps[i2][0:64, 0:nb * 128].rearrange("p (b d) -> p b d", d=128), func=AF.Copy),
                    reads=[bPT[i2]], writes=[bv64])
            tiles = ([("c", i) for i in range(CTX // 64)] if need_ctx else []) + [("x", r) for r in range(R)]
            for kind, r in tiles:
                i2 = it % 2
                it += 1
                if kind == "x":
                    q0 = CTX + r * 64
                    rs = min(max(r - KH // 2, 0), R - KH)
                    delta = r - rs
                    segs = [(CTX + rs * 64, WK), (0, CTX)]
                else:
                    q0 = r * 64
                    segs = [(0, CTX)]
                nk = sum(s[1] for s in segs)
                off = 0
                for (k0, kn) in segs:
                    for c0 in range(0, kn, 512):
                        cn = min(512, kn - c0)
                        k.op("pe", lambda e, i2=i2, off=off, c0=c0, cn=cn, k0=k0, q0=q0: e.matmul(
                            S_ps[i2][:, off + c0:off + c0 + cn], lhsT=qT[:, q0:q0 + 64], rhs=kT[:, k0 + c0:k0 + c0 + cn],
                            start=True, stop=True), reads=[bq, bk], writes=[bS[i2]])
                    off += kn
                if kind == "x":
                    bsl = Bh[:, 7 - delta + (8 - KH):7 - delta + 8, :] if KH == 8 else Bh[:, 7 - delta:7 - delta + KH, :]
                    k.op("dve", lambda e, i2=i2, bsl=bsl: e.tensor_tensor(
                        out=Ssb[i2][:, 0:WK].rearrange("p (a c) -> p a c", c=64),
                        in0=S_ps[i2][:, 0:WK].rearrange("p (a c) -> p a c", c=64), in1=bsl, op=ALU.add),
                        reads=[bS[i2], bB], writes=[bSs[i2]])
                    k.op("act", lambda e, i2=i2: e.activation(out=Ssb[i2][:, WK:WK + CTX], in_=S_ps[i2][:, WK:WK + CTX], func=AF.Copy),
                         reads=[bS[i2]], writes=[bSs[i2]])
                else:
                    k.op("act", lambda e, i2=i2: e.activation(out=Ssb[i2][:, 0:CTX], in_=S_ps[i2][:, 0:CTX], func=AF.Copy),
                         reads=[bS[i2]], writes=[bSs[i2]])
                k.op("dve", lambda e, i2=i2, nk=nk: e.tensor_reduce(out=st1[i2][:, 0:1], in_=Ssb[i2][:, 0:nk], axis=AX.X, op=ALU.max, negate=True),
                     reads=[bSs[i2]], writes=[bst[i2]])
                k.op("act", lambda e, i2=i2, nk=nk: e.activation(out=Ssb[i2][:, 0:nk], in_=Ssb[i2][:, 0:nk], func=AF.Exp,
                                                                  bias=st1[i2][:, 0:1], scale=1.0, accum_out=st1[i2][:, 1:2]),
                     reads=[bSs[i2], bst[i2]], writes=[bSs[i2], bst[i2]])
                k.op("dve", lambda e, i2=i2: e.reciprocal(out=st1[i2][:, 2:3], in_=st1[i2][:, 1:2]), reads=[bst[i2]], writes=[bst[i2]])
                k.op("dve", lambda e, i2=i2, nk=nk: e.tensor_scalar(out=Pn[i2][:, 0:nk], in0=Ssb[i2][:, 0:nk], scalar1=st1[i2][:, 2:3],
                                                                    scalar2=None, op0=ALU.mult),
                     reads=[bSs[i2], bst[i2]], writes=[bPn[i2]])
                nblk = nk // 64
                for b in range(nblk):
                    k.op("pe", lambda e, i2=i2, b=b: e.transpose(PT_ps[i2][0:64, b * 64:(b + 1) * 64], Pn[i2][:, b * 64:(b + 1) * 64],
                                                                  identb[0:64, 0:64]),
                         reads=[bPn[i2], p.c["buf"]], writes=[bPT[i2]])
                k.op("act", lambda e, i2=i2, nk=nk: e.activation(out=PTs[i2][:, 0:nk], in_=PT_ps[i2][0:64, 0:nk], func=AF.Copy),
                     reads=[bPT[i2]], writes=[bPTs[i2]])
                b = 0
                for (k0, kn) in segs:
                    for bb in range(kn // 64):
                        tokb = k0 // 64 + bb
                        k.op("pe", lambda e, i2=i2, b=b, tokb=tokb, nblk=nblk: e.matmul(
                            O_ps[i2][:, 0:64], lhsT=v64[:, tokb, :], rhs=PTs[i2][:, b * 64:(b + 1) * 64],
                            start=(b == 0), stop=(b == nblk - 1)), reads=[bv64, bPTs[i2]], writes=[bO[i2]])
                        b += 1
                k.op("act", lambda e, i2=i2, q0=q0: e.activation(out=oT[:, q0:q0 + 64], in_=O_ps[i2][:, 0:64], func=AF.Copy),
                     reads=[bO[i2]], writes=[boT])
            k.dma(o_loc.ap()[h * 128:(h + 1) * 128, :], oT[:], reads=[boT], writes=[bo], queue="pool")
    k.barrier()


import os as _os
SC_SKIP = set(_os.environ.get('SCSKIP', '').split(','))


def phase_scan(p, kind, qkvT, bqkv, qrow0, krow0, vrow0, nheads, dk, dv, gate_src, odir, bodir):
    k, cfg = p.k, p.cfg
    T, CTX = cfg["T"], cfg["CTX"]
    NCH = T // 128
    NCC = CTX // 128
    dkc, EB = dk // 128, dv // 128
    EBA = EB + (1 if kind == "ml" else 0)
    c = p.c
    k.barrier()
    with ExitStack() as st:
        qs = k.sb("sc_q", [128, dkc, T], BF16, st)
        ks = k.sb("sc_k", [128, dkc, T], BF16, st)
        vs = k.sb("sc_v", [128, EB, T], BF16, st)
        bq, bk, bv = Buf(), Buf(), Buf()
        gtab = k.sb("sc_g", [128, NCH, 4], F32, st)
        bg = Buf()
        Cm = k.sb("sc_Cm", [128, dkc, EBA * 128], F32, st)
        Cb = k.sb("sc_Cb", [128, dkc, EBA * 128], BF16, st)
        bCm, bCb = Buf(), Buf()
        R = k.sb("sc_R", [128, 128], F32, st)
        Ebc = k.sb("sc_E", [128, 128], F32, st)
        WT = k.sb("sc_W", [128, 128], F32, st)
        PT = k.sb("sc_P", [128, 128], BF16, st)
        qd = k.sb("sc_qd", [128, dkc, 128], BF16, st)
        kd = k.sb("sc_kd", [128, dkc, 128], BF16, st)
        vt = k.sb("sc_vt", [128, EBA, 128], BF16, st)
        stc = k.sb("sc_st", [128, 8], F32, st)
        osb = [k.sb("sc_o%d" % i, [128, EB, 128], F32, st) for i in range(2)]
        bR, bE, bW, bP, bqd, bkd, bvt, bst = [Buf() for _ in range(8)]
        bos = [Buf() for _ in range(2)]
        Bps = k.ps("sc_Bps", [128, 512], F32, st)
        STp = k.ps("sc_STp", [128, 512], F32, st)
        Ops = k.ps("sc_Ops", [128, 1024], F32, st)
        dCp = k.ps("sc_dCp", [128, 1024], F32, st)
        TRp = k.ps("sc_TRp", [128, 1024], BF16, st)
        TRv = k.ps("sc_TRv", [128, 1024], BF16, st)
        bBps, bSTp, bOps, bdCp, bTRp, bTRv = [Buf() for _ in range(6)]
        if kind == "ml":
            k.op("pool", lambda e: e.memset(vt[:, EB, :], 1.0), writes=[bvt])
        if gate_src[0] == "ret":
            H2 = 2 * nheads
            lg = k.sb("sc_lg", [128, H2], F32, st)
            blg = Buf()
            k.dma(lg[:], gate_src[1].to_broadcast([128, H2]), writes=[blg])
            k.op("act", lambda e: e.activation(out=lg[:], in_=lg[:], func=AF.Exp, scale=-float(np.log(2.0))), reads=[blg], writes=[blg])
            k.op("dve", lambda e: e.tensor_scalar(out=lg[:], in0=lg[:], scalar1=-1.0, scalar2=1.0, op0=ALU.mult, op1=ALU.add), reads=[blg], writes=[blg])
            k.op("act", lambda e: e.activation(out=lg[:], in_=lg[:], func=AF.Ln), reads=[blg], writes=[blg])
        else:
            gT = k.sb("sc_gT", [4, T], F32, st)
            bgc = k.sb("sc_bgc", [4, 1], F32, st)
            gtmp = k.sb("sc_gtmp", [128, NCH, 2], F32, st)
            bgT = Buf()
        step_i = 0
        for h in range(nheads):
            k.dma(qs[:], qkvT.ap()[qrow0 + h * dk:qrow0 + (h + 1) * dk, :].rearrange("(c p) t -> p c t", p=128), reads=[bqkv], writes=[bq])
            k.dma(ks[:], qkvT.ap()[krow0 + h * dk:krow0 + (h + 1) * dk, :].rearrange("(c p) t -> p c t", p=128), reads=[bqkv], writes=[bk])
            k.dma(vs[:], qkvT.ap()[vrow0 + h * dv:vrow0 + (h + 1) * dv, :].rearrange("(c p) t -> p c t", p=128), reads=[bqkv], writes=[bv])
            if gate_src[0] == "ret":
                k.op("pool", lambda e: e.memset(gtab[:], 0.0), writes=[bg])
                for d_ in range(2):
                    col = d_ * nheads + h
                    k.op("dve", lambda e, d_=d_, col=col: e.tensor_scalar(
                        out=gtab[:, :, 2 * d_ + 1], in0=gtab[:, :, 2 * d_ + 1], scalar1=lg[:, col:col + 1], scalar2=None, op0=ALU.add),
                        reads=[bg, blg], writes=[bg])
            else:
                gsrc, bgs, bgate_ap = gate_src[1], gate_src[2], gate_src[3]
                for gi in range(4):
                    row = gi * nheads + h
                    k.dma(gT[gi:gi + 1, :], gsrc.ap()[row:row + 1, :], reads=[bgs], writes=[bgT])
                    k.dma(bgc[gi:gi + 1, :], bgate_ap[row:row + 1, :], writes=[bgT])
                k.op("dve", lambda e: e.tensor_scalar(out=gT[:], in0=gT[:], scalar1=bgc[:, 0:1], scalar2=None, op0=ALU.add), reads=[bgT], writes=[bgT])
                for ch in range(NCH):
                    k.op("pe", lambda e, ch=ch: e.transpose(Bps[:, ch * 4:(ch + 1) * 4], gT[0:4, ch * 128:(ch + 1) * 128], c["ident"][0:4, 0:4]),
                         reads=[bgT, c["buf"]], writes=[bBps])
                k.op("dve", lambda e: e.tensor_copy(out=gtab[:], in_=Bps[:, 0:NCH * 4].rearrange("p (a b) -> p a b", b=4)), reads=[bBps], writes=[bg])
                for d_ in range(2):
                    k.op("act", lambda e, d_=d_: e.activation(out=gtmp[:, :, d_], in_=gtab[:, :, 2 * d_ + 1], func=AF.Exp, scale=-1.0), reads=[bg], writes=[bgT])
                k.op("dve", lambda e: e.tensor_scalar(out=gtmp[:], in0=gtmp[:], scalar1=1.0, scalar2=None, op0=ALU.add), reads=[bgT], writes=[bgT])
                k.op("act", lambda e: e.activation(out=gtmp[:], in_=gtmp[:], func=AF.Ln), reads=[bgT], writes=[bgT])
                for d_ in range(2):
                    k.op("dve", lambda e, d_=d_: e.tensor_scalar(out=gtab[:, :, 2 * d_ + 1], in0=gtmp[:, :, d_], scalar1=-1.0, scalar2=None, op0=ALU.mult),
                         reads=[bgT], writes=[bg])
            for d_ in range(2):
                order = list(range(NCC)) + list(range(NCC, NCH))
                if d_ == 1:
                    order = list(range(NCC - 1, -1, -1)) + list(range(NCH - 1, NCC - 1, -1))
                tri = c["triu"] if d_ == 0 else c["tril"]
                endcol = 127 if d_ == 0 else 0
                k.op("pool", lambda e: e.memset(Cm[:], 0.0), writes=[bCm])
                k.op("pool", lambda e: e.memset(Cb[:], 0.0), writes=[bCb])
                for ch in order:
                    t0 = ch * 128
                    igc = gtab[:, ch, 2 * d_:2 * d_ + 1]
                    lfc = gtab[:, ch, 2 * d_ + 1:2 * d_ + 2]
                    o2 = step_i % 2
                    step_i += 1
                    k.op("dve", lambda e, lfc=lfc, tri=tri: e.tensor_scalar(out=R[:], in0=tri[:], scalar1=lfc, scalar2=None, op0=ALU.mult),
                         reads=[bg, c["buf"]], writes=[bR])
                    k.op("pe", lambda e: e.matmul(Bps[:, 0:128], lhsT=c["onesf"][:], rhs=R[:], start=True, stop=True), reads=[bR, c["buf"]], writes=[bBps])
                    k.op("pe", lambda e, lfc=lfc, tri=tri: e.matmul(Bps[:, 128:129], lhsT=tri[:], rhs=lfc, start=True, stop=True), reads=[bg, c["buf"]], writes=[bBps])
                    k.op("dve", lambda e, igc=igc: e.tensor_tensor(out=stc[:, 0:1], in0=igc, in1=Bps[:, 128:129], op=ALU.subtract), reads=[bg, bBps], writes=[bst])
                    k.op("dve", lambda e, endcol=endcol: e.tensor_tensor(out=stc[:, 1:2], in0=stc[:, 0:1], in1=Bps[:, endcol:endcol + 1], op=ALU.add), reads=[bst, bBps], writes=[bst])
                    k.op("act", lambda e: e.activation(out=stc[:, 2:3], in_=stc[:, 1:2], func=AF.Exp), reads=[bst], writes=[bst])
                    k.op("act", lambda e, endcol=endcol: e.activation(out=stc[:, 3:4], in_=Bps[:, endcol:endcol + 1], func=AF.Exp), reads=[bBps, bst], writes=[bst])
                    k.op("act", lambda e: e.activation(out=Ebc[:], in_=Bps[:, 0:128], func=AF.Exp), reads=[bBps], writes=[bE])
                    k.op("act", lambda e: e.activation(out=WT[:], in_=Bps[:, 0:128], func=AF.Exp, bias=stc[:, 0:1], scale=1.0), reads=[bBps, bst], writes=[bW])
                    if d_ == 0:
                        k.op("pool", lambda e: e.affine_select(out=WT[:], in_=WT[:], pattern=[[1, 128]], compare_op=ALU.is_ge, fill=0.0, base=0, channel_multiplier=-1), reads=[bW], writes=[bW])
                    else:
                        k.op("pool", lambda e: e.affine_select(out=WT[:], in_=WT[:], pattern=[[-1, 128]], compare_op=ALU.is_ge, fill=0.0, base=0, channel_multiplier=1), reads=[bW], writes=[bW])
                    for dc in (range(dkc) if ("A" not in SC_SKIP and "Ak" not in SC_SKIP) else []):
                        k.op("pe", lambda e, dc=dc, t0=t0: e.transpose(TRp[:, dc * 128:(dc + 1) * 128], ks[:, dc, t0:t0 + 128], c["identb"][:]), reads=[bk, c["buf"]], writes=[bTRp])
                    for eb in (range(EB) if ("A" not in SC_SKIP and "Av" not in SC_SKIP) else []):
                        k.op("pe", lambda e, eb=eb, t0=t0: e.transpose(TRv[:, eb * 128:(eb + 1) * 128], vs[:, eb, t0:t0 + 128], c["identb"][:]), reads=[bv, c["buf"]], writes=[bTRv])
                    if "Ake" not in SC_SKIP:
                        k.op("act", lambda e: e.activation(out=kd[:], in_=TRp[:, 0:dkc * 128].rearrange("p (a b) -> p a b", b=128), func=AF.Identity, scale=stc[:, 2:3]), reads=[bTRp, bst], writes=[bkd])
                    if "Ave" not in SC_SKIP:
                        k.op("dve", lambda e: e.tensor_copy(out=vt[:, 0:EB, :], in_=TRv[:, 0:EB * 128].rearrange("p (a b) -> p a b", b=128)), reads=[bTRv], writes=[bvt])
                    for dc in (range(dkc) if "B" not in SC_SKIP else []):
                        k.op("pe", lambda e, dc=dc, t0=t0: e.matmul(STp[:, 0:128], lhsT=ks[:, dc, t0:t0 + 128], rhs=qs[:, dc, t0:t0 + 128], start=(dc == 0), stop=(dc == dkc - 1)), reads=[bk, bq], writes=[bSTp])
                    k.op("dve", lambda e: e.tensor_tensor(out=PT[:], in0=STp[:, 0:128], in1=WT[:], op=ALU.mult), reads=[bSTp, bW], writes=[bP])
                    k.op("dve", lambda e, t0=t0: e.tensor_tensor(out=qd[:], in0=qs[:, :, t0:t0 + 128], in1=Ebc[:].unsqueeze(1).to_broadcast([128, dkc, 128]), op=ALU.mult), reads=[bq, bE], writes=[bqd])
                    for eb in (range(EBA) if "C" not in SC_SKIP else []):
                        k.op("pe", lambda e, eb=eb: e.matmul(Ops[:, eb * 128:(eb + 1) * 128], lhsT=vt[:, eb, :], rhs=PT[:], start=True, stop=False), reads=[bvt, bP], writes=[bOps])
                        for dc in range(dkc):
                            k.op("pe", lambda e, eb=eb, dc=dc: e.matmul(Ops[:, eb * 128:(eb + 1) * 128], lhsT=Cb[:, dc, eb * 128:(eb + 1) * 128], rhs=qd[:, dc, :], start=False, stop=(dc == dkc - 1)), reads=[bCb, bqd], writes=[bOps])
                    if kind == "ml":
                        k.op("act", lambda e: e.activation(out=WT[:], in_=Ops[:, EB * 128:(EB + 1) * 128], func=AF.Abs), reads=[bOps, bW, bP], writes=[bW])
                        k.op("dve", lambda e: e.tensor_scalar(out=WT[:], in0=WT[:], scalar1=1.0, scalar2=None, op0=ALU.max), reads=[bW], writes=[bW])
                        k.op("dve", lambda e: e.reciprocal(out=WT[:], in_=WT[:]), reads=[bW], writes=[bW])
                        k.op("dve", lambda e, o2=o2: e.tensor_tensor(out=osb[o2][:], in0=Ops[:, 0:EB * 128].rearrange("p (a b) -> p a b", b=128), in1=WT[:].unsqueeze(1).to_broadcast([128, EB, 128]), op=ALU.mult), reads=[bOps, bW], writes=[bos[o2]])
                    else:
                        k.op("act", lambda e, o2=o2: e.activation(out=osb[o2][:], in_=Ops[:, 0:EB * 128].rearrange("p (a b) -> p a b", b=128), func=AF.Copy), reads=[bOps], writes=[bos[o2]])
                    k.dma(odir[d_].ap()[h * dv:(h + 1) * dv, t0:t0 + 128].rearrange("(a p) t -> p a t", p=128), osb[o2][:], reads=[bos[o2]], writes=[bodir[d_]], queue="pool")
                    for dc in (range(dkc) if "D" not in SC_SKIP else []):
                        for c0 in range(0, EBA * 128, 512):
                            cn = min(512, EBA * 128 - c0)
                            k.op("pe", lambda e, dc=dc, c0=c0, cn=cn: e.matmul(dCp[:, c0:c0 + cn], lhsT=kd[:, dc, :], rhs=vt[:].rearrange("p a b -> p (a b)")[:, c0:c0 + cn], start=True, stop=True), reads=[bkd, bvt], writes=[bdCp])
                        k.op("dve", lambda e, dc=dc: e.scalar_tensor_tensor(out=Cm[:, dc, :], in0=Cm[:, dc, :], scalar=stc[:, 3:4], in1=dCp[:, 0:EBA * 128], op0=ALU.mult, op1=ALU.add), reads=[bCm, bst, bdCp], writes=[bCm])
                        k.op("act", lambda e, dc=dc: e.activation(out=Cb[:, dc, :], in_=Cm[:, dc, :], func=AF.Copy), reads=[bCm], writes=[bCb])
    k.barrier()


def phase_combine(p, kind, odir, bodir, gT, bgT, grow0, nheads, dv, o_out, bo_out, ng_ap=None):
    k, cfg = p.k, p.cfg
    T = cfg["T"]
    EB = dv // 128
    k.barrier()
    with ExitStack() as st:
        a = [k.sb("cb_a%d" % i, [128, EB, 512], F32, st) for i in range(2)]
        b = [k.sb("cb_b%d" % i, [128, EB, 512], F32, st) for i in range(2)]
        g = [k.sb("cb_g%d" % i, [128, EB, 512], F32, st) for i in range(2)]
        sq = [k.sb("cb_s%d" % i, [128, EB, 512], BF16, st) for i in range(2)]
        ob = [k.sb("cb_o%d" % i, [128, EB, 512], BF16, st) for i in range(2)]
        rt = [k.sb("cb_r%d" % i, [128, 512], F32, st) for i in range(2)]
        ba, bb, bgg, bsq, bob, brt = [[Buf() for _ in range(2)] for _ in range(6)]
        pt = [k.ps("cb_p%d" % i, [128, 512], F32, st) for i in range(2)]
        bpt = [Buf() for _ in range(2)]
        ngt = None
        if ng_ap is not None:
            ngt = k.sb("cb_ng", [128, nheads * EB], F32, st)
            bng = Buf()
            k.dma(ngt[:], ng_ap, writes=[bng])
        it = 0
        for h in range(nheads):
            for (t0, n) in tok_blocks(cfg):
                i2 = it % 2
                it += 1
                rows = slice(h * dv, (h + 1) * dv)
                k.dma(a[i2][:, :, 0:n], odir[0].ap()[rows, t0:t0 + n].rearrange("(e p) t -> p e t", p=128), reads=[bodir[0]], writes=[ba[i2]])
                k.dma(b[i2][:, :, 0:n], odir[1].ap()[rows, t0:t0 + n].rearrange("(e p) t -> p e t", p=128), reads=[bodir[1]], writes=[bb[i2]])
                k.dma(g[i2][:, :, 0:n], gT.ap()[grow0 + h * dv:grow0 + (h + 1) * dv, t0:t0 + n].rearrange("(e p) t -> p e t", p=128), reads=[bgT], writes=[bgg[i2]])
                k.op("dve", lambda e, i2=i2, n=n: e.tensor_tensor(out=a[i2][:, :, 0:n], in0=a[i2][:, :, 0:n], in1=b[i2][:, :, 0:n], op=ALU.add), reads=[ba[i2], bb[i2]], writes=[ba[i2]])
                k.op("act", lambda e, i2=i2, n=n: e.activation(out=sq[i2][:, :, 0:n], in_=a[i2][:, :, 0:n], func=AF.Square), reads=[ba[i2]], writes=[bsq[i2]])
                for eb in range(EB):
                    k.op("pe", lambda e, i2=i2, n=n, eb=eb: e.matmul(pt[i2][:, 0:n], lhsT=p.c["onesb"][:], rhs=sq[i2][:, eb, 0:n], start=(eb == 0), stop=(eb == EB - 1)), reads=[bsq[i2], p.c["buf"]], writes=[bpt[i2]])
                k.op("act", lambda e, i2=i2, n=n: e.activation(out=rt[i2][:, 0:n], in_=pt[i2][:, 0:n], func=AF.Sqrt, bias=p.c["epsc"][:], scale=1.0 / dv), reads=[bpt[i2], p.c["buf"]], writes=[brt[i2]])
                k.op("dve", lambda e, i2=i2, n=n: e.reciprocal(out=rt[i2][:, 0:n], in_=rt[i2][:, 0:n]), reads=[brt[i2]], writes=[brt[i2]])
                k.op("dve", lambda e, i2=i2, n=n: e.tensor_tensor(out=a[i2][:, :, 0:n], in0=a[i2][:, :, 0:n], in1=rt[i2][:, 0:n].unsqueeze(1).to_broadcast([128, EB, n]), op=ALU.mult), reads=[ba[i2], brt[i2]], writes=[ba[i2]])
                k.op("act", lambda e, i2=i2, n=n: e.activation(out=b[i2][:, :, 0:n], in_=g[i2][:, :, 0:n], func=AF.Sigmoid), reads=[bgg[i2], bb[i2], ba[i2]], writes=[bb[i2]])
                if kind == "ret":
                    k.op("pool", lambda e, i2=i2, n=n: e.tensor_tensor(out=b[i2][:, :, 0:n], in0=b[i2][:, :, 0:n], in1=g[i2][:, :, 0:n], op=ALU.mult), reads=[bb[i2], bgg[i2]], writes=[bb[i2]])
                else:
                    for eb in range(EB):
                        k.op("pool", lambda e, i2=i2, n=n, eb=eb, h=h: e.tensor_scalar(out=b[i2][:, eb, 0:n], in0=b[i2][:, eb, 0:n], scalar1=ngt[:, h * EB + eb:h * EB + eb + 1], scalar2=None, op0=ALU.mult), reads=[bb[i2], bng], writes=[bb[i2]])
                k.op("dve", lambda e, i2=i2, n=n: e.tensor_tensor(out=ob[i2][:, :, 0:n], in0=a[i2][:, :, 0:n], in1=b[i2][:, :, 0:n], op=ALU.mult), reads=[ba[i2], bb[i2]], writes=[bob[i2]])
                k.dma(o_out.ap()[rows, t0:t0 + n].rearrange("(e p) t -> p e t", p=128), ob[i2][:, :, 0:n], reads=[bob[i2]], writes=[bo_out], queue="pool")
    k.barrier()


import os
DBG_SKIP = set(os.environ.get('KSKIP', '').split(','))


GEMM_MAXM_RES = 16


def mixer_kind(l):
    return l % 3


def win_cols(cfg, l):
    kind = mixer_kind(l)
    D = cfg["D"]
    if kind == 0:
        return 3 * D
    if kind == 1:
        return 4 * D
    return 2 * cfg["ML_QK"] + 2 * D


def build_program(cfg):
    nc = bass.Bass("TRN2", target_bir_lowering=False)
    L = cfg["L"]
    D, T, KC, CTX, SEQ = [cfg[x] for x in ("D", "T", "KC", "CTX", "SEQ")]
    with ExitStack() as stack:
        k = K(nc, stack)
        p = P(nc, k, cfg)
        xs = p.inp("xs", [D, T])
        p.inp("n1g", [L, 128, KC])
        p.inp("n2g", [L, 128, KC])
        fing = p.inp("fing", [128, KC])
        p.inp("convw", [L, 128, 2 * KC, 3])
        p.inp("convb", [L, 128, 2 * KC])
        n_a = len(range(0, L, 3))
        n_r = len(range(1, L, 3))
        n_m = len(range(2, L, 3))
        p.inp("rpbT", [n_a, cfg["NA_HEADS"], 64, 15 * 64])
        p.inp("namask", [64, 64])
        dk_r = cfg["RET_DK"]
        dkc_r = dk_r // 128
        if n_r:
            ret_decay = p.inp("ret_decay", [n_r, 1, 2 * cfg["RET_HEADS"]])
            ropeP = p.inp("ropeP", [128, dkc_r, 128])
            ropeC = p.inp("ropeC", [dk_r, SEQ])
            ropeS = p.inp("ropeS", [dk_r, SEQ])
        if n_m:
            mlwg = p.inp("mlwg", [n_m, D, 4 * cfg["ML_HEADS"]])
            mlbg = p.inp("mlbg", [n_m, 4 * cfg["ML_HEADS"], 1])
            mlng = p.inp("mlng", [n_m, 128, KC])
        wins = [p.inp("win%d" % l, [D, win_cols(cfg, l)]) for l in range(L)]
        wos = [p.inp("wo%d" % l, [D, D]) for l in range(L)]
        wups = [p.inp("wup%d" % l, [D, 2 * D]) for l in range(L)]
        wdns = [p.inp("wdn%d" % l, [D, D]) for l in range(L)]
        out = nc.dram_tensor("out", [D, SEQ], F32, kind="ExternalOutput")
        build_consts(p)
        phase_adaln(p)
        slabs = [k.dram("slab%d" % i, [D, T], F32) for i in range(2)]
        bslabs = [Buf() for _ in range(2)]
        stats = k.dram("stats", [1, T], F32)
        bstats = Buf()
        hT = k.dram("hT", [D, T], BF16)
        bhT = Buf()
        qkvT = k.dram("qkvT", [3 * D, T], BF16)
        bqkv = Buf()
        gT = k.dram("gT", [D, T], F32)
        bgT = Buf()
        gatesT = k.dram("gatesT", [128, T], F32)
        bgates = Buf()
        o_t = k.dram("o_t", [D, T], BF16)
        bo = Buf()
        u_t = [k.dram("u_a", [D, T], F32), k.dram("u_g", [D, T], F32)]
        bu = Buf()
        f_t = k.dram("f_t", [D, T], BF16)
        bf = Buf()
        odir = [k.dram("odir%d" % i, [D, T], F32) for i in range(2)]
        bodir = [Buf() for _ in range(2)]
        cur = 0
        slabs[0] = xs
        p.dbg.update(qkvT=(qkvT, bqkv), o_t=(o_t, bo), hT=(hT, bhT), stats=(stats, bstats), mod=(p.mod, p.b_mod),
                     u_t=(u_t[0], bu), f_t=(f_t, bf), odir0=(odir[0], bodir[0]), odir1=(odir[1], bodir[1]), gT=(gT, bgT),
                     gatesT=(gatesT, bgates))
        spare = k.dram("slab2", [D, T], F32)
        for l in range(L):
            kind = mixer_kind(l)
            last = (l == L - 1)
            j = l // 3
            k.barrier()
            with ExitStack() as lst:
                vec = layer_vectors(p, l, lst)
                ncols = win_cols(cfg, l)
                bias1 = k.sb("bias1", [128, ncols // 128, 2], F32, lst)
                biasg = k.sb("biasg", [128, 1, 2], F32, lst)
                bias2 = k.sb("bias2", [128, 2 * KC, 2], F32, lst)
                phase_prep(p, slabs[cur], bslabs[cur], stats, bstats, hT, bhT, vec["gmod1"])
                with ExitStack() as rst:
                    rs, brs = make_rstd_cache(p, rst, stats, bstats, D)
                    with ExitStack() as est:
                        rope = None
                        tb_hook = None
                        if kind == 0:
                            s_q = 128.0 ** -0.5
                            rsf = lambda mb: s_q if mb < KC else 1.0
                            dst_fn = lambda mb: (qkvT, bqkv, mb * 128, True)
                        elif kind == 1:
                            s_k = float(dk_r) ** -0.5
                            rsf = lambda mb: s_k if KC <= mb < 2 * KC else 1.0
                            dst_fn = lambda mb: (qkvT, bqkv, mb * 128, True) if mb < 3 * KC else (gT, bgT, (mb - 3 * KC) * 128, False)
                            Pm = k.sb("ropePm", [128, dkc_r, 128], BF16, est)
                            Pf = k.sb("ropePf", [128, dkc_r, 128], F32, est)
                            bPm = Buf()
                            k.dma(Pf[:], ropeP.ap(), writes=[bPm])
                            k.op("dve", lambda e: e.tensor_copy(out=Pm[:], in_=Pf[:]), reads=[bPm], writes=[bPm])
                            ct = [k.sb("ropeC%d" % i, [128, dkc_r, 512], F32, est) for i in range(2)]
                            sn = [k.sb("ropeS%d" % i, [128, dkc_r, 512], F32, est) for i in range(2)]
                            btab = [Buf() for _ in range(2)]
                            rps = k.ps("rope_ps", [128, 512], F32, est)
                            brps = Buf()
                            rstate = {"i": 0, "cur": 0}

                            def tb_hook(t0, n):
                                if t0 < CTX:
                                    return
                                i2 = rstate["i"] % 2
                                rstate["i"] += 1
                                rstate["cur"] = i2
                                k.dma(ct[i2][:, :, 0:n], ropeC.ap()[:, t0 - CTX:t0 - CTX + n].rearrange("(c p) t -> p c t", p=128), writes=[btab[i2]])
                                k.dma(sn[i2][:, :, 0:n], ropeS.ap()[:, t0 - CTX:t0 - CTX + n].rearrange("(c p) t -> p c t", p=128), writes=[btab[i2]])

                            def rope_apply(rc, t0, n, of_, ob_, tt_, bo_):
                                i2 = rstate["cur"]
                                k.op("act", lambda e: e.activation(out=ob_[:, 0:n], in_=of_[:, 0:n], func=AF.Copy), reads=[bo_], writes=[bo_])
                                k.op("pe", lambda e: e.matmul(rps[:, 0:n], lhsT=Pm[:, rc, :], rhs=ob_[:, 0:n], start=True, stop=True), reads=[bo_, bPm], writes=[brps])
                                k.op("dve", lambda e: e.tensor_tensor(out=tt_[:, 0:n], in0=of_[:, 0:n], in1=ct[i2][:, rc, 0:n], op=ALU.mult), reads=[bo_, btab[i2]], writes=[bo_])
                                k.op("dve", lambda e: e.tensor_tensor(out=of_[:, 0:n], in0=rps[:, 0:n], in1=sn[i2][:, rc, 0:n], op=ALU.mult), reads=[brps, btab[i2], bo_], writes=[bo_])
                                k.op("dve", lambda e: e.tensor_tensor(out=ob_[:, 0:n], in0=tt_[:, 0:n], in1=of_[:, 0:n], op=ALU.add), reads=[bo_], writes=[bo_])
                            rope = dict(is_rope=lambda mb: (mb % dkc_r) if mb < 2 * KC else None, apply=rope_apply)
                            if "rope" in DBG_SKIP:
                                rope = None
                        else:
                            QKC = cfg["ML_QK"] // 128
                            s_k = float(cfg["ML_DK"]) ** -0.5
                            rsf = lambda mb: s_k if QKC <= mb < 2 * QKC else 1.0
                            dst_fn = lambda mb: (qkvT, bqkv, mb * 128, True) if mb < 2 * QKC + KC else (gT, bgT, (mb - 2 * QKC - KC) * 128, False)
                        epi = epi_norm_factory(p, est, rs, brs, bias1, dst_fn, row_scale=rsf, tag="e1", rope=rope)
                        phase_gemm(p, wins[l].ap(), ncols, hT, bhT, epi, bias_vecs=vec["shb1"], bias_out=bias1, tb_hook=tb_hook)
                    if kind == 2:
                        with ExitStack() as est:
                            epi = epi_norm_factory(p, est, rs, brs, biasg, lambda mb: (gatesT, bgates, 0, False), tag="eg")
                            phase_gemm(p, mlwg.ap()[j], 4 * cfg["ML_HEADS"], hT, bhT, epi, bias_vecs=vec["shb1"], bias_out=biasg)
                    k.barrier()
                if kind == 0:
                    phase_natten(p, j, qkvT, bqkv, o_t, bo, not last)
                elif kind == 1:
                    H = cfg["RET_HEADS"]
                    if "scan" not in DBG_SKIP:
                        phase_scan(p, "ret", qkvT, bqkv, 0, D, 2 * D, H, dk_r, D // H, ("ret", ret_decay.ap()[j]), odir, bodir)
                    if "combine" not in DBG_SKIP:
                        phase_combine(p, "ret", odir, bodir, gT, bgT, 0, H, D // H, o_t, bo)
                else:
                    H = cfg["ML_HEADS"]
                    QK = cfg["ML_QK"]
                    phase_scan(p, "ml", qkvT, bqkv, 0, QK, 2 * QK, H, cfg["ML_DK"], cfg["ML_DV"],
                               ("ml", gatesT, bgates, mlbg.ap()[j]), odir, bodir)
                    phase_combine(p, "ml", odir, bodir, gT, bgT, 0, H, cfg["ML_DV"], o_t, bo, ng_ap=mlng.ap()[j])
                def next_slab(cur_t):
                    for cand in (slabs[1], spare):
                        if cand is not cur_t:
                            return cand
                nxt_t = next_slab(slabs[cur])
                nb = Buf()
                with ExitStack() as est:
                    epi = epi_resid_factory(p, est, vec["gate1"], slabs[cur], bslabs[cur], nxt_t, nb, tag="e2")
                    phase_gemm(p, wos[l].ap(), D, o_t, bo, epi, maxm=GEMM_MAXM_RES)
                slabs[cur], bslabs[cur] = nxt_t, nb
                phase_prep(p, slabs[cur], bslabs[cur], stats, bstats, hT, bhT, vec["gmod2"])
                with ExitStack() as rst:
                    rs2, brs2 = make_rstd_cache(p, rst, stats, bstats, D)
                    with ExitStack() as est:
                        epi = epi_norm_factory(p, est, rs2, brs2, bias2, lambda mb: (u_t[mb // KC], bu, (mb % KC) * 128, False), tag="e3")
                        phase_gemm(p, wups[l].ap(), 2 * D, hT, bhT, epi, bias_vecs=vec["shb2"], bias_out=bias2)
                    k.barrier()
                phase_convgate(p, l, u_t, bu, f_t, bf)
                nxt_t = next_slab(slabs[cur])
                nb = Buf()
                with ExitStack() as est:
                    epi = epi_resid_factory(p, est, vec["gate2"], slabs[cur], bslabs[cur], nxt_t, nb, tag="e4")
                    phase_gemm(p, wdns[l].ap(), D, f_t, bf, epi, maxm=GEMM_MAXM_RES)
                slabs[cur], bslabs[cur] = nxt_t, nb
                k.barrier()
        phase_prep(p, slabs[cur], bslabs[cur], stats, bstats)
        with ExitStack() as st:
            rs, brs = make_rstd_cache(p, st, stats, bstats, D)
            fg = k.sb("fg", [128, KC], F32, st)
            bfgv = Buf()
            k.dma(fg[:], fing.ap(), writes=[bfgv])
            xt = [k.sb("fx%d" % i, [128, 512], F32, st) for i in range(2)]
            bxt = [Buf() for _ in range(2)]
            bout = Buf()
            it = 0
            for mb in range(KC):
                for t0 in range(CTX, T, 512):
                    i2 = it % 2
                    it += 1
                    k.dma(xt[i2][:], slabs[cur].ap()[mb * 128:(mb + 1) * 128, t0:t0 + 512], reads=[bslabs[cur]], writes=[bxt[i2]])
                    k.op("dve", lambda e, i2=i2, mb=mb, t0=t0: e.scalar_tensor_tensor(
                        out=xt[i2][:], in0=xt[i2][:], scalar=fg[:, mb:mb + 1], in1=rs[:, t0:t0 + 512], op0=ALU.mult, op1=ALU.mult),
                        reads=[bxt[i2], bfgv, brs], writes=[bxt[i2]])
                    k.dma(out.ap()[mb * 128:(mb + 1) * 128, t0 - CTX:t0 - CTX + 512], xt[i2][:], reads=[bxt[i2]], writes=[bout], queue="pool")
            k.barrier()
        if DEBUG_DUMPS:
            for nm, (t, b) in p.dbg.items():
                o = nc.dram_tensor("dbg_" + nm, list(t.shape), t.dtype, kind="ExternalOutput")
                k.dma(o.ap(), t.ap(), reads=[b], writes=[Buf()], queue="sp")
        k.finish()
    return nc, p


def fm(vec, KC):
    return np.ascontiguousarray(np.asarray(vec, np.float32).reshape(KC, 128).T)


def rope_tables(cfg):
    dk, SEQ = cfg["RET_DK"], cfg["SEQ"]
    half = dk // 4
    inv = (10000.0 ** (-np.arange(half, dtype=np.float32) / np.float32(half))).astype(np.float32)
    pos = np.arange(SEQ)
    row = (pos // cfg["GW"]).astype(np.float32)
    col = (pos % cfg["GW"]).astype(np.float32)
    C = np.zeros((dk, SEQ), np.float32)
    S = np.zeros((dk, SEQ), np.float32)
    dkc = dk // 128
    Pm = np.zeros((128, dkc, 128), np.float32)
    for f in range(dk):
        sec = f // (dk // 2)
        w = f % (dk // 2)
        i = w % half
        is2 = w >= half
        ang = (row if sec == 0 else col) * inv[i]
        C[f] = np.cos(ang)
        S[f] = np.sin(ang)
        partner = f - half if is2 else f + half
        assert partner // 128 == f // 128
        Pm[partner % 128, f // 128, f % 128] = 1.0 if is2 else -1.0
    return Pm, C, S


def pack_inputs(cfg, inp):
    L = cfg["L"]
    D, T, KC, CTX, SEQ = [cfg[x] for x in ("D", "T", "KC", "CTX", "SEQ")]
    f32 = np.float32
    m = {}
    m["xs"] = np.ascontiguousarray(np.concatenate([np.asarray(inp["ctx"], f32)[0].T, np.asarray(inp["x"], f32)[0].T], axis=1))
    m["cvec"] = np.ascontiguousarray(np.stack([fm(inp["c"][0], KC), fm(inp["c_ctx"], KC)], axis=-1))
    m["n1g"] = np.stack([fm(inp["norm1_g"][l], KC) for l in range(L)])
    m["n2g"] = np.stack([fm(inp["norm2_g"][l], KC) for l in range(L)])
    m["fing"] = fm(inp["final_g"], KC)
    m["adaw"] = np.asarray(inp["ada_w"], f32)
    m["adab"] = np.ascontiguousarray(np.asarray(inp["ada_b"], f32)[:, None, :])
    cw = np.asarray(inp["ffn_conv_w"], f32)
    m["convw"] = np.ascontiguousarray(cw.reshape(L, 3, 2 * KC, 128).transpose(0, 3, 2, 1))
    m["convb"] = np.ascontiguousarray(np.asarray(inp["ffn_conv_b"], f32).reshape(L, 2 * KC, 128).transpose(0, 2, 1))
    qcol = np.arange(64)[:, None]
    kcol = np.arange(64)[None, :]
    cs = np.clip(qcol - 8, 0, 64 - 16)
    col_ok = (kcol >= cs) & (kcol < cs + 16)
    dc_idx = np.clip(kcol - qcol + 15, 0, 30)
    m["namask"] = np.where(col_ok, 0.0, -30000.0).astype(f32)
    rpb = np.asarray(inp["na_rpb"], f32)
    g = rpb[:, :, :, dc_idx]
    m["rpbT"] = np.ascontiguousarray(g.transpose(0, 1, 3, 2, 4).reshape(rpb.shape[0], rpb.shape[1], 64, 15 * 64))
    if len(range(1, L, 3)):
        m["ret_decay"] = np.ascontiguousarray(np.asarray(inp["ret_decay"], f32).reshape(-1, 1, 2 * cfg["RET_HEADS"]))
        Pm, C, S = rope_tables(cfg)
        m["ropeP"], m["ropeC"], m["ropeS"] = Pm, C, S
    if len(range(2, L, 3)):
        m["mlwg"] = np.asarray(inp["ml_w_gate"], f32)
        m["mlbg"] = np.ascontiguousarray(np.asarray(inp["ml_b_gate"], f32)[:, :, None])
        m["mlng"] = np.stack([fm(inp["ml_norm_g"][j], KC) for j in range(np.asarray(inp["ml_norm_g"]).shape[0])])
    for l in range(L):
        kind = mixer_kind(l)
        j = l // 3
        if kind == 0:
            m["win%d" % l] = np.asarray(inp["na_w_qkv"], f32)[j]
            m["wo%d" % l] = np.asarray(inp["na_w_o"], f32)[j]
        elif kind == 1:
            m["win%d" % l] = np.asarray(inp["ret_w_in"], f32)[j]
            m["wo%d" % l] = np.asarray(inp["ret_w_o"], f32)[j]
        else:
            m["win%d" % l] = np.asarray(inp["ml_w_in"], f32)[j]
            m["wo%d" % l] = np.asarray(inp["ml_w_o"], f32)[j]
        m["wup%d" % l] = np.asarray(inp["ffn_w_up"], f32)[l]
        m["wdn%d" % l] = np.asarray(inp["ffn_w_down"], f32)[l]
    return m


LAST_RES = None


def run_cfg(cfg, inputs):
    global LAST_RES
    nc, p = build_program(cfg)
    m = pack_inputs(cfg, inputs)
    res = run_bass_kernel_spmd(nc, [m], core_ids=[0])
    LAST_RES = res.results
    outT = np.asarray(res.results[0]["out"])
    return np.ascontiguousarray(outT.T)[None].astype(np.float32)


def kernel(**inputs):
    cfg = make_cfg()
    return run_cfg(cfg, inputs)
```
